# Optimizing a Trainium2 kernel written in Bass

```python
import jax, jax.numpy as jnp
from jax import lax
import numpy as np

D_MODEL = 1024
BATCH = 8
SEQ = 2048
DEPTH = 1

HEAD_DIM = 64
A_HEADS = 8
A_KV_HEADS = 2
B_HEADS = 8
B_KV_HEADS = 2
A_WIDTH = A_HEADS * HEAD_DIM
A_KV_WIDTH = A_KV_HEADS * HEAD_DIM
B_WIDTH = B_HEADS * HEAD_DIM
B_KV_WIDTH = B_KV_HEADS * HEAD_DIM
WINDOW = 128
BLOCK = 128
GRID_W = 64
ROPE_THETA = 10000.0
QK_EPS = 1e-6
LN_EPS = 1e-5
DN_ALPHA = (2.0 * DEPTH) ** 0.25
DN_BETA = (8.0 * DEPTH) ** -0.25

IN_SPLITS = (A_WIDTH, A_KV_WIDTH, A_KV_WIDTH, A_WIDTH,
             B_WIDTH, B_KV_WIDTH, B_KV_WIDTH, B_WIDTH,
             D_MODEL, D_MODEL)
IN_WIDTH = sum(IN_SPLITS)
IN_OFFSETS = tuple(int(v) for v in np.cumsum(IN_SPLITS)[:-1])
V_COLUMNS = (2, 6)

kernel_name = "hybrid_gated_window_axial_gqa_deepnorm"


def layer_norm(x, g, b):
    xf = x.astype(jnp.float32)
    mu = jnp.mean(xf, axis=-1, keepdims=True)
    var = jnp.mean(jnp.square(xf - mu), axis=-1, keepdims=True)
    y = (xf - mu) * lax.rsqrt(var + LN_EPS) * g.astype(jnp.float32) + b.astype(jnp.float32)
    return y.astype(x.dtype)


def rms_norm_heads(t, g):
    tf = t.astype(jnp.float32)
    y = tf * lax.rsqrt(jnp.mean(jnp.square(tf), axis=-1, keepdims=True) + QK_EPS) * g.astype(jnp.float32)
    return y.astype(t.dtype)


def axial_rope(t, row_idx, col_idx):
    half = HEAD_DIM // 2
    axis_pairs = half // 2
    freqs = ROPE_THETA ** (-jnp.arange(axis_pairs, dtype=jnp.float32) / axis_pairs)
    ang = jnp.concatenate([row_idx[:, None] * freqs, col_idx[:, None] * freqs], axis=-1)
    cos = jnp.cos(ang)[None, :, None, :]
    sin = jnp.sin(ang)[None, :, None, :]
    tf = t.astype(jnp.float32)
    t1, t2 = tf[..., :half], tf[..., half:]
    out = jnp.concatenate([t1 * cos - t2 * sin, t2 * cos + t1 * sin], axis=-1)
    return out.astype(t.dtype)


def windowed_sink_attention(q, k, v, sink):
    bsz, seq, _, dh = q.shape
    nblk = seq // BLOCK
    grp = A_HEADS // A_KV_HEADS
    qb = q.reshape(bsz, nblk, BLOCK, A_KV_HEADS, grp, dh)

    def band(t):
        tb = t.reshape(bsz, nblk, BLOCK, A_KV_HEADS, dh)
        tp = jnp.pad(tb, ((0, 0), (1, 1), (0, 0), (0, 0), (0, 0)))
        return jnp.concatenate([tp[:, :-2], tp[:, 1:-1], tp[:, 2:]], axis=2)

    kw, vw = band(k), band(v)
    s = jnp.einsum('bnqkgd,bnskd->bnkgqs', qb, kw).astype(jnp.float32) * (dh ** -0.5)

    blk = jnp.arange(nblk)
    q_pos = blk[:, None] * BLOCK + jnp.arange(BLOCK)[None, :]
    k_pos = (blk[:, None] - 1) * BLOCK + jnp.arange(3 * BLOCK)[None, :]
    dist = jnp.abs(q_pos[:, :, None] - k_pos[:, None, :])
    valid = (dist <= WINDOW) & (k_pos[:, None, :] >= 0) & (k_pos[:, None, :] < seq)

    slopes = jnp.exp2(-8.0 * (jnp.arange(A_HEADS, dtype=jnp.float32) + 1.0) / A_HEADS)
    slopes = slopes.reshape(A_KV_HEADS, grp)[None, None, :, :, None, None]
    s = s - slopes * dist.astype(jnp.float32)[None, :, None, None, :, :]
    s = jnp.where(valid[None, :, None, None, :, :], s, -jnp.inf)

    sink_l = sink.astype(jnp.float32).reshape(A_KV_HEADS, grp)[None, None, :, :, None, None]
    m = jnp.maximum(jnp.max(s, axis=-1, keepdims=True), sink_l)
    p = jnp.exp(s - m)
    denom = jnp.sum(p, axis=-1, keepdims=True) + jnp.exp(sink_l - m)
    p = (p / denom).astype(v.dtype)
    out = jnp.einsum('bnkgqs,bnskd->bnqkgd', p, vw)
    return out.reshape(bsz, seq, A_HEADS * dh)


def blockwise_global_attention(q, k, v):
    bsz, seq, _, dh = q.shape
    nblk = seq // BLOCK
    grp = B_HEADS // B_KV_HEADS
    qb = q.reshape(bsz, nblk, BLOCK, B_KV_HEADS, grp, dh).transpose(1, 0, 2, 3, 4, 5)
    scale = dh ** -0.5

    def one_block(q_blk):
        s = jnp.einsum('bqkgd,bskd->bkgqs', q_blk, k).astype(jnp.float32) * scale
        p = jax.nn.softmax(s, axis=-1).astype(v.dtype)
        return jnp.einsum('bkgqs,bskd->bqkgd', p, v)

    out = lax.map(one_block, qb)
    return out.transpose(1, 0, 2, 3, 4, 5).reshape(bsz, seq, B_HEADS * dh)


def hybrid_layer(x, w_in, b_gate, sink_a, qnorm_b, knorm_b, w_proj_a, w_proj_b, w_out, ln_g, ln_b,
                 row_idx, col_idx):
    bsz, seq, _ = x.shape
    h = x @ w_in
    qa, ka, va, za, qb, kb, vb, zb, ga, gb = jnp.split(h, IN_OFFSETS, axis=-1)

    ya = windowed_sink_attention(qa.reshape(bsz, seq, A_HEADS, HEAD_DIM),
                                 ka.reshape(bsz, seq, A_KV_HEADS, HEAD_DIM),
                                 va.reshape(bsz, seq, A_KV_HEADS, HEAD_DIM), sink_a)
    ya = (ya * jax.nn.silu(za)) @ w_proj_a

    qh = axial_rope(rms_norm_heads(qb.reshape(bsz, seq, B_HEADS, HEAD_DIM), qnorm_b), row_idx, col_idx)
    kh = axial_rope(rms_norm_heads(kb.reshape(bsz, seq, B_KV_HEADS, HEAD_DIM), knorm_b), row_idx, col_idx)
    yb = blockwise_global_attention(qh, kh, vb.reshape(bsz, seq, B_KV_HEADS, HEAD_DIM))
    yb = (yb * jax.nn.silu(zb)) @ w_proj_b

    gate_a = jax.nn.sigmoid(ga + b_gate[:D_MODEL])
    gate_b = jax.nn.sigmoid(gb + b_gate[D_MODEL:])
    y = (gate_a * ya + gate_b * yb) @ w_out

    return layer_norm(DN_ALPHA * x + y, ln_g, ln_b)


def setup_inputs(seed: int = 0) -> dict:
    key = jax.random.key(seed)
    ks = jax.random.split(key, 12)
    x = jax.random.normal(ks[0], (BATCH, SEQ, D_MODEL), jnp.float32)
    col_scale = jnp.concatenate([
        jnp.full((w,), DN_BETA if i in V_COLUMNS else 1.0, jnp.float32)
        for i, w in enumerate(IN_SPLITS)])
    w_in = jax.random.normal(ks[1], (DEPTH, D_MODEL, IN_WIDTH), jnp.float32) * (D_MODEL ** -0.5) * col_scale
    b_gate = 0.02 * jax.random.normal(ks[2], (DEPTH, 2 * D_MODEL), jnp.float32)
    sink_a = 0.5 * jax.random.normal(ks[3], (DEPTH, A_HEADS), jnp.float32)
    qnorm_b = 1.0 + 0.02 * jax.random.normal(ks[4], (DEPTH, HEAD_DIM), jnp.float32)
    knorm_b = 1.0 + 0.02 * jax.random.normal(ks[5], (DEPTH, HEAD_DIM), jnp.float32)
    w_proj_a = jax.random.normal(ks[6], (DEPTH, A_WIDTH, D_MODEL), jnp.float32) * (A_WIDTH ** -0.5) * DN_BETA
    w_proj_b = jax.random.normal(ks[7], (DEPTH, B_WIDTH, D_MODEL), jnp.float32) * (B_WIDTH ** -0.5) * DN_BETA
    w_out = jax.random.normal(ks[8], (DEPTH, D_MODEL, D_MODEL), jnp.float32) * (D_MODEL ** -0.5) * DN_BETA
    ln_g = 1.0 + 0.02 * jax.random.normal(ks[9], (DEPTH, D_MODEL), jnp.float32)
    ln_b = 0.02 * jax.random.normal(ks[10], (DEPTH, D_MODEL), jnp.float32)
    return {"x": x, "w_in": w_in, "b_gate": b_gate, "sink_a": sink_a, "qnorm_b": qnorm_b,
            "knorm_b": knorm_b, "w_proj_a": w_proj_a, "w_proj_b": w_proj_b, "w_out": w_out,
            "ln_g": ln_g, "ln_b": ln_b}


def reference(x, w_in, b_gate, sink_a, qnorm_b, knorm_b, w_proj_a, w_proj_b, w_out, ln_g, ln_b):
    seq = x.shape[1]
    rows = seq // GRID_W
    grid_r, grid_c = jnp.meshgrid(jnp.arange(rows), jnp.arange(GRID_W), indexing='ij')
    row_idx = grid_r.reshape(-1).astype(jnp.float32)
    col_idx = grid_c.reshape(-1).astype(jnp.float32)
    h = x
    for layer in range(DEPTH):
        h = hybrid_layer(h, w_in[layer], b_gate[layer], sink_a[layer], qnorm_b[layer], knorm_b[layer],
                         w_proj_a[layer], w_proj_b[layer], w_out[layer], ln_g[layer], ln_b[layer],
                         row_idx, col_idx)
    return h
```

```python
import contextlib

import numpy as np

import concourse.bass as bass
import concourse.mybir as mybir
from concourse.bass_utils import run_bass_kernel_spmd

F32 = mybir.dt.float32
BF16 = mybir.dt.bfloat16
F32R = mybir.dt.float32r
AF = mybir.ActivationFunctionType
ALU = mybir.AluOpType

SEQ = 2048
DM = 1024
QK_EPS = 1e-6
LN_EPS = 1e-5
DN_ALPHA = 2.0 ** 0.25
MASK_NEG = -4000.0
DBGF = {}

ENGS = ("pe", "act", "dve", "pool", "sp")
DMA_POOL = {"sp": 12, "pool": 12, "act": 4}


class Op:
    __slots__ = ("idx", "eng", "fn", "dma", "deps", "signal", "cnt", "pos", "dsem", "dval")

    def __init__(self, idx, eng, fn, dma):
        self.idx = idx
        self.eng = eng
        self.fn = fn
        self.dma = dma
        self.deps = ()
        self.signal = False
        self.cnt = 0
        self.pos = 0
        self.dsem = None
        self.dval = 0


class Sched:
    def __init__(self, nc):
        self.nc = nc
        self.ops = []
        self.last_w = {}
        self.rd_eng = {}
        self.rd_dma = {}

    def op(self, eng, fn, reads=(), writes=(), dma=False):
        o = Op(len(self.ops), eng, fn, dma)
        deps = set()
        for r in reads:
            w = self.last_w.get(r)
            if w is not None:
                deps.add(w)
        for k in writes:
            w = self.last_w.get(k)
            if w is not None:
                deps.add(w)
            deps.update(self.rd_eng.get(k, {}).values())
            deps.update(self.rd_dma.get(k, ()))
        o.deps = tuple(sorted(deps))
        for r in reads:
            if dma:
                self.rd_dma.setdefault(r, []).append(o.idx)
            else:
                self.rd_eng.setdefault(r, {})[eng] = o.idx
        for k in writes:
            self.last_w[k] = o.idx
            self.rd_eng[k] = {}
            self.rd_dma[k] = []
        self.ops.append(o)
        return o

    def _need_sync(self, p, o):
        if p.eng == o.eng and not o.dma:
            if p.eng == "pe":
                return False
            if o.pos - p.pos > 2:
                return False
        return True

    def emit(self, es):
        nc = self.nc
        ops = self.ops
        streams = {e: [] for e in ENGS}
        for o in ops:
            o.pos = len(streams[o.eng])
            streams[o.eng].append(o)
        for o in ops:
            for d in o.deps:
                p = ops[d]
                if p.dma:
                    continue
                if self._need_sync(p, o):
                    p.signal = True
        for e in ENGS:
            c = 0
            for o in streams[e]:
                if o.dma:
                    continue
                if o.signal:
                    c += 1
                o.cnt = c
        esem = {e: es.enter_context(nc.semaphore(f"es_{e}")) for e in ("pe", "act", "dve", "pool")}
        dsems = {q: [es.enter_context(nc.semaphore(f"ds_{q}{i}")) for i in range(n)]
                 for q, n in DMA_POOL.items()}
        for q, n in DMA_POOL.items():
            k = 0
            for o in streams[q]:
                if o.dma:
                    o.dsem = (q, k % n)
                    o.dval = 16 * (k // n + 1)
                    k += 1

        def run_stream(e, engobj):
            waited = {}

            def wait(semkey, sem, val):
                if waited.get(semkey, 0) >= val:
                    return
                waited[semkey] = val
                engobj.wait_ge(sem, val)

            for o in streams[e]:
                for d in o.deps:
                    p = ops[d]
                    if p.dma:
                        wait(("d",) + p.dsem, dsems[p.dsem[0]][p.dsem[1]], p.dval)
                    elif self._need_sync(p, o):
                        wait(("e", p.eng), esem[p.eng], p.cnt)
                if o.dma:
                    sem = dsems[o.dsem[0]][o.dsem[1]]
                    if o.dval > 16:
                        wait(("d",) + o.dsem, sem, o.dval - 16)
                    o.fn(engobj).then_inc(sem, 16)
                else:
                    ins = o.fn(engobj)
                    if o.signal:
                        ins.then_inc(esem[e], 1)
            if e in DMA_POOL:
                last = {}
                for o in streams[e]:
                    if o.dma:
                        last[o.dsem] = o.dval
                for k, v in last.items():
                    wait(("d",) + k, dsems[k[0]][k[1]], v)

        block = es.enter_context(nc.Block())

        @block.sync
        def _(eng):
            run_stream("sp", eng)

        @block.tensor
        def _(eng):
            run_stream("pe", eng)

        @block.scalar
        def _(eng):
            run_stream("act", eng)

        @block.vector
        def _(eng):
            run_stream("dve", eng)

        @block.gpsimd
        def _(eng):
            run_stream("pool", eng)


def build_program(dbg=None):
    nc = bass.Bass("TRN2", target_bir_lowering=False)
    d_xT = nc.dram_tensor("xT", [DM, SEQ], F32, kind="ExternalInput").ap()
    d_x = nc.dram_tensor("x", [SEQ, DM], F32, kind="ExternalInput").ap()
    d_win = nc.dram_tensor("win", [DM, 4608], F32, kind="ExternalInput").ap()
    d_wpa = nc.dram_tensor("wpa", [512, DM], F32, kind="ExternalInput").ap()
    d_wpb = nc.dram_tensor("wpb", [512, DM], F32, kind="ExternalInput").ap()
    d_wo = nc.dram_tensor("wo", [DM, DM], F32, kind="ExternalInput").ap()
    d_cvec = nc.dram_tensor("cvec", [128, 32], F32, kind="ExternalInput").ap()
    d_lng = nc.dram_tensor("lng", [128, DM], F32, kind="ExternalInput").ap()
    d_lnb = nc.dram_tensor("lnb", [128, DM], F32, kind="ExternalInput").ap()
    d_cos = nc.dram_tensor("cos", [128, SEQ], F32, kind="ExternalInput").ap()
    d_sin = nc.dram_tensor("sin", [128, SEQ], F32, kind="ExternalInput").ap()
    d_bias = nc.dram_tensor("biasA", [128, 8 * 384], F32, kind="ExternalInput").ap()
    d_mats = nc.dram_tensor("mats", [128, 384], F32, kind="ExternalInput").ap()
    d_out = nc.dram_tensor("out", [SEQ, DM], F32, kind="ExternalOutput").ap()

    d_win_v = d_win.rearrange("(kc p) c -> p kc c", p=128)
    d_xT_v = d_xT.rearrange("(kc p) t -> p kc t", p=128)

    with contextlib.ExitStack() as es:
        def sb(name, shape, dt):
            return es.enter_context(nc.sbuf_tensor(name, shape, dt))

        XT = sb("XT", [128, 8, SEQ], BF16)
        SZ = sb("SZ", [128, 8, SEQ], BF16)
        QR = sb("QR", [128, 24576], BF16)
        WROT = sb("WROT", [128, 2, 8, 512], BF16)
        CW = sb("CW", [128, 8192], F32)
        TT = sb("TT", [128, 6144], F32)
        LNG = sb("LNG", [128, DM], F32)
        LNB = sb("LNB", [128, DM], F32)
        CV = sb("CV", [128, 32], F32)
        ESINK = sb("ESINK", [128, 4], F32)
        MATS = sb("MATS", [128, 256], F32)
        ONES = sb("ONES", [128, 64], BF16)
        STT = sb("STT", [128, 2, 12], F32)
        MV = sb("MV", [128, 2, 2], F32)
        SM = sb("SM", [128, 2, 2], F32)
        SCR = sb("SCR", [128, 8], F32)
        SM2 = sb("SM2", [128, 2, 1], F32)
        CONSTP = sb("CONSTP", [128, 2], F32)
        TSQ = sb("TSQ", [128, 2, 512], F32)
        PS = es.enter_context(nc.psum_tensor("PS", [128, 8, 512], F32))

        QA = QR[:, 0:8192].rearrange("p (c t) -> p c t", c=4)
        QB = QR[:, 8192:16384].rearrange("p (c t) -> p c t", c=4)
        KA = QR[:, 16384:18432]
        KB = QR[:, 18432:20480]
        V = QR[:, 20480:24576].rearrange("p (t f) -> p t f", t=16)
        MT = QR[:, 0:16384].rearrange("p (c t) -> p c t", c=8)
        COS = CW[:, 0:2048]
        SINS = CW[:, 2048:4096]
        T1 = [CW[:, 4096 + 512 * i: 4096 + 512 * (i + 1)] for i in range(8)]
        WPA = CW[:, 0:2048].bitcast(BF16).rearrange("p (c o) -> p c o", c=4)
        WPB = CW[:, 2048:4096].bitcast(BF16).rearrange("p (c o) -> p c o", c=4)
        WO = CW[:, 4096:8192].bitcast(BF16).rearrange("p (c o) -> p c o", c=8)
        BIAS = TT[:, 0:1536].bitcast(BF16).rearrange("p (h q) -> p h q", h=8)
        PTA = TT[:, 1536:3072].bitcast(BF16).rearrange("p (s h q) -> p s h q", s=4, h=2)
        PTB = TT[:, 3072:4608].bitcast(BF16).rearrange("p (s h q) -> p s h q", s=3, h=2)
        E2 = [TT[:, 4608 + 512 * i: 4608 + 512 * (i + 1)] for i in range(2)]
        PTA2 = TT[:, 3072:4608].bitcast(BF16).rearrange("p (s h q) -> p s h q", s=4, h=2)
        G3 = [TT[:, 512 * i: 512 * (i + 1)] for i in range(8)]
        X4 = [TT[:, 4096:5120], TT[:, 5120:6144]]
        QRT = QR[:, 16384:24576].bitcast(F32)
        R4 = [QRT[:, 1024 * i: 1024 * (i + 1)] for i in range(3)]
        O4 = [QRT[:, 3072:4096]]
        MATSR = sb("MATSR", [128, 256], F32)
        BONES = MATSR[:, 0:128]
        PERM = MATSR[:, 128:256]

        s = Sched(nc)
        RG_CW, RG_TT, RG_QR = ("RG", "CW"), ("RG", "TT"), ("RG", "QR")
        RG_PTB = ("RG", "PTB")

        def tsl(n, w=512):
            return slice(n * w, (n + 1) * w)

        def mm(out, lhsT, rhs, start, stop, reads, writes):
            s.op("pe", lambda e: e.matmul(out, lhsT=lhsT, rhs=rhs, start=start, stop=stop),
                 reads=reads, writes=writes)

        def act(out, in_, func, reads, writes, bias=None, scale=None):
            kw = {}
            if bias is not None:
                kw["bias"] = bias
            if scale is not None:
                kw["scale"] = scale
            s.op("act", lambda e: e.activation(out=out, in_=in_, func=func, **kw), reads=reads, writes=writes)

        def tt(eng, out, in0, in1, op, reads, writes):
            s.op(eng, lambda e: e.tensor_tensor(out=out, in0=in0, in1=in1, op=op), reads=reads, writes=writes)

        sw_n = [0]

        def switch(rg):
            k = sw_n[0]
            sw_n[0] += 1
            s.op("dve", lambda e: e.memset(SCR[:, k:k + 1], 0.0), writes=[rg])

        s.op("sp", lambda e: e.dma_start(out=CV[:], in_=d_cvec), writes=[("CV",)], dma=True)
        s.op("sp", lambda e: e.dma_start(out=MATS[:], in_=d_mats[:, 128:384]), writes=[("MATS0",)], dma=True)
        s.op("dve", lambda e: e.tensor_copy(out=MATSR[:], in_=MATS[:]), reads=[("MATS0",)], writes=[("MATS",)])
        def load_xt(n):
            for hf in range(2):
                s.op("pool", (lambda hf: lambda e: e.dma_start(out=XT[:, 4 * hf:4 * hf + 4, n * 512:(n + 1) * 512],
                                                          in_=d_xT_v[:, 4 * hf:4 * hf + 4, n * 512:(n + 1) * 512]))(hf),
                     writes=[("XT", n, hf)], dma=True)

        def load_w(g, slot):
            for hf in range(2):
                s.op("pool", (lambda hf: lambda e: e.dma_start(out=WROT[:, slot, 4 * hf:4 * hf + 4, :],
                                                          in_=d_win_v[:, 4 * hf:4 * hf + 4, g * 512:(g + 1) * 512]))(hf),
                     writes=[("W", slot, hf)], dma=True)

        WSEQ = [3, 4, 2, 1, 0, 5, 6, 7, 8]
        load_w(WSEQ[0], 0)
        load_xt(0)
        load_xt(1)
        load_w(WSEQ[1], 1)
        load_xt(2)
        load_xt(3)
        s.op("sp", lambda e: e.dma_start(out=COS, in_=d_cos), reads=[RG_CW], writes=[("COS",)], dma=True)
        s.op("sp", lambda e: e.dma_start(out=SINS, in_=d_sin), reads=[RG_CW], writes=[("SIN",)], dma=True)
        s.op("sp", lambda e: e.dma_start(out=LNG[:], in_=d_lng), writes=[("LNG",)], dma=True)
        s.op("sp", lambda e: e.dma_start(out=LNB[:], in_=d_lnb), writes=[("LNB",)], dma=True)
        BIASF = TT[:, 0:1536].bitcast(BF16)
        for hf in range(2):
            s.op("pool", (lambda hf: lambda e: e.dma_start(out=BIASF[:, hf * 1536:(hf + 1) * 1536],
                                                          in_=d_bias[:, hf * 1536:(hf + 1) * 1536]))(hf),
                 reads=[RG_TT], writes=[("BIAS", hf)], dma=True)
        s.op("pool", lambda e: e.memset(ONES[:], 1.0), writes=[("ONES",)])
        s.op("pool", lambda e: e.memset(CONSTP[:, 0:1], LN_EPS), writes=[("CONSTP0",)])
        s.op("pool", lambda e: e.memset(CONSTP[:, 1:2], -0.5), reads=[("CONSTP0",)], writes=[("CONSTP",)])
        act(ESINK[:], CV[:, 16:20], AF.Exp, reads=[("CV",)], writes=[("ESINK",)])

        bank_rr = [0]
        pipe_rr = [0]

        def next_bank():
            b = bank_rr[0] % 4
            bank_rr[0] += 1
            return b

        def evac_rms(bank, n, gcol, dst, dst_key):
            inst = pipe_rr[0] % 2
            pipe_rr[0] += 1
            tsq, tu, tb = TSQ[:, inst, :], T1[3 * inst + 1], T1[3 * inst + 2]
            ksq, ku, kb_ = ("TSQ", inst), ("T1", 3 * inst + 1), ("T1", 3 * inst + 2)
            bm, br = 4 + inst, 6 + inst
            pb = PS[:, bank, :]
            act(tsq, pb, AF.Square, reads=[("PS", bank), RG_CW], writes=[ksq])
            s.op("dve", lambda e: e.tensor_scalar_mul(out=tu, in0=pb, scalar1=CV[:, gcol:gcol + 1]),
                 reads=[("PS", bank), ("CV",), RG_CW, ksq], writes=[ku])
            tc_, kc_ = T1[6 + inst], ("T1", 6 + inst)
            tt("pool", tc_, tu, COS[:, tsl(n)], ALU.mult, reads=[ku, ("COS",), RG_CW], writes=[kc_])

            def part2():
                mm(PS[:, bm, :], BONES, tsq, True, True, reads=[("MATS",), ksq], writes=[("PS", bm)])
                mm(PS[:, br, :], PERM, tu, True, True, reads=[("MATS",), ku], writes=[("PS", br)])
                act(tsq, PS[:, bm, :], AF.Ln, reads=[("PS", bm), RG_CW], writes=[ksq], bias=QK_EPS, scale=1.0)
                act(tsq, tsq, AF.Exp, reads=[ksq], writes=[ksq], scale=-0.5)
                tt("dve", tb, PS[:, br, :], SINS[:, tsl(n)], ALU.mult, reads=[("PS", br), ("SIN",), RG_CW], writes=[kb_])
                tt("dve", tc_, tc_, tb, ALU.add, reads=[kc_, kb_, RG_CW], writes=[kc_])
                tt("pool", dst, tc_, tsq, ALU.mult, reads=[kc_, ksq, RG_QR, RG_CW], writes=[dst_key])

            rms_pending.append(part2)

        rms_pending = []

        def flush_rms(keep):
            while len(rms_pending) > keep:
                rms_pending.pop(0)()

        def inproj_chunk(slot, f, n, kind, c):
            bank = next_bank()
            for kc in range(8):
                mm(PS[:, bank, :], WROT[:, slot, kc, f * 128:(f + 1) * 128], XT[:, kc, tsl(n)],
                   kc == 0, kc == 7, reads=[("W", slot, kc // 4), ("XT", n, kc // 4)], writes=[("PS", bank)])
            flush_rms(0)
            pb = PS[:, bank, :]
            if kind == "qa":
                dst = QA[:, c, tsl(n)]
                s.op("dve", lambda e: e.tensor_copy(out=dst, in_=pb), reads=[("PS", bank), RG_QR],
                     writes=[("QA", c, n)])
            elif kind == "ka":
                dst = KA[:, tsl(n)]
                s.op("dve", lambda e: e.tensor_copy(out=dst, in_=pb), reads=[("PS", bank), RG_QR],
                     writes=[("KA", n)])
            elif kind in ("za", "zb"):
                m = 0 if kind == "za" else 1
                act(SZ[:, m * 4 + c, tsl(n)], pb, AF.Silu, reads=[("PS", bank)], writes=[("SZ", m * 4 + c, n)])
            elif kind == "qb":
                if DBGF.get("norms"):
                    dst = QB[:, c, tsl(n)]
                    s.op("dve", lambda e: e.tensor_copy(out=dst, in_=pb), reads=[("PS", bank), RG_QR],
                         writes=[("QB", c, n)])
                else:
                    evac_rms(bank, n, 20, QB[:, c, tsl(n)], ("QB", c, n))
            elif kind == "kb":
                if DBGF.get("norms"):
                    dst = KB[:, tsl(n)]
                    s.op("dve", lambda e: e.tensor_copy(out=dst, in_=pb), reads=[("PS", bank), RG_QR],
                         writes=[("KB", n)])
                else:
                    evac_rms(bank, n, 21, KB[:, tsl(n)], ("KB", n))

        group_kinds = {
            0: [("qb", 0), ("qb", 1), ("qb", 2), ("qb", 3)],
            1: [("kb", 0), ("ka", 0), None, None],
            2: [("qa", 0), ("qa", 1), ("qa", 2), ("qa", 3)],
            3: [("za", 0), ("za", 1), ("za", 2), ("za", 3)],
            4: [("zb", 0), ("zb", 1), ("zb", 2), ("zb", 3)],
        }
        for pos in range(5):
            g = WSEQ[pos]
            slot = pos % 2
            if pos >= 1:
                load_w(WSEQ[pos + 1], (pos + 1) % 2)
            for n in range(4):
                for f in range(4):
                    kd = group_kinds[g][f]
                    if kd is None:
                        continue
                    inproj_chunk(slot, f, n, kd[0], kd[1])
            if g == 1:
                flush_rms(0)
                for t in range(16):
                    bank = next_bank()
                    for kc in range(8):
                        mm(PS[:, bank, 0:256], XT[:, kc, tsl(t, 128)], WROT[:, slot, kc, 256:512],
                           kc == 0, kc == 7, reads=[("W", slot, kc // 4), ("XT", t // 4, kc // 4)], writes=[("PS", bank)])
                    s.op("dve", (lambda t, bank: lambda e: e.tensor_copy(out=V[:, t, :], in_=PS[:, bank, 0:256]))(t, bank),
                         reads=[("PS", bank), RG_QR], writes=[("V", t)])

        flush_rms(0)

        def dump(name, ap, shape, dt):
            d = nc.dram_tensor(name, shape, dt, kind="ExternalOutput").ap()
            s.op("sp", lambda e: e.dma_start(out=d, in_=ap), reads=list(s.last_w.keys()), dma=True)

        if dbg == 1:
            dump("dQR", QR[:], [128, 24576], BF16)
            dump("dSZ", SZ[:].rearrange("p c t -> p (c t)"), [128, 8 * SEQ], BF16)
            s.emit(es)
            return nc
        switch(RG_CW)
        s.op("pool", lambda e: e.dma_start(out=WPA, in_=d_wpa.rearrange("(kc p) o -> p kc o", p=128)),
             reads=[RG_CW], writes=[("WPA",)], dma=True)
        s.op("pool", lambda e: e.dma_start(out=WPB, in_=d_wpb.rearrange("(kc p) o -> p kc o", p=128)),
             reads=[RG_CW], writes=[("WPB",)], dma=True)
        s.op("pool", lambda e: e.dma_start(out=WO, in_=d_wo.rearrange("(kc p) o -> p kc o", p=128)),
             reads=[RG_CW], writes=[("WO",)], dma=True)

        ep_rr = [0]

        def epilogue(ob, db, m, c, Q, sink):
            e0, e1 = E2[0], E2[1]
            k0 = [("E2", 0, 0), ("E2", 0, 1)]
            k1 = [("E2", 1, 0), ("E2", 1, 1)]
            s.op("dve", lambda e: e.reciprocal(out=e0, in_=PS[:, db, :]), reads=[("PS", db), RG_TT], writes=k0)
            tt("dve", e1, PS[:, ob, :], e0, ALU.mult, reads=[("PS", ob), RG_TT] + k0, writes=k1)
            dst = SZ[:, m * 4 + c, tsl(Q)]
            tt("dve", dst, e1, dst, ALU.mult, reads=k1 + [("SZ", m * 4 + c, Q), RG_TT], writes=[("SZ", m * 4 + c, Q)])

        step = [0]
        ep_pending = []
        ep2 = [0]
        RING = [(PTA, "PTA"), (PTA2, "PTA2")]

        def a_scores(p, c, j):
            qs = max(j - 1, 0) * 128
            qe = min(j + 2, 16) * 128
            N = qe - qs
            boff = qs - (j - 1) * 128
            for h2 in range(2):
                rows = slice(h2 * 64, h2 * 64 + 64)
                bk = 2 * p + h2
                mm(PS[:, bk, 0:N], KA[rows, tsl(j, 128)], QA[rows, c, qs:qe], True, True,
                   reads=[("KA", j // 4), RG_QR] + [("QA", c, n) for n in range(qs // 512, (qe - 1) // 512 + 1)],
                   writes=[("PS", bk)])
            return N, boff

        def a_exp(p, c, j, N, boff):
            ring, rk = RING[p]
            slot = j % 4
            pt = ring[:, slot, :, 0:N]
            act(pt, PS[:, 2 * p:2 * p + 2, 0:N], AF.Exp, reads=[("PS", 2 * p), ("PS", 2 * p + 1), RG_TT, RG_PTB],
                writes=[(rk, slot)], scale=0.125)
            tt("dve", pt, pt, BIAS[:, 2 * c:2 * c + 2, boff:boff + N], ALU.mult,
               reads=[(rk, slot), ("BIAS", c // 2), RG_TT, RG_PTB], writes=[(rk, slot)])

        def epilogue_a(bank, c, q2):
            k = ep2[0] % 2
            ep2[0] += 1
            e0 = E2[0][:, k * 256:(k + 1) * 256]
            e1 = E2[1][:, k * 256:(k + 1) * 256]
            k0, k1 = ("E2", 0, k), ("E2", 1, k)
            act(e0, PS[:, bank, 256:512], AF.Ln, reads=[("PS", bank), ("ESINK",), RG_TT], writes=[k0],
                bias=ESINK[:, c:c + 1], scale=1.0)
            act(e0, e0, AF.Exp, reads=[k0], writes=[k0], scale=-1.0)
            tt("dve", e1, PS[:, bank, 0:256], e0, ALU.mult, reads=[("PS", bank), k0, RG_TT], writes=[k1])
            dst = SZ[:, c, q2 * 256:(q2 + 1) * 256]
            tt("dve", dst, e1, dst, ALU.mult, reads=[k1, ("SZ", c, q2 // 2), RG_TT], writes=[("SZ", c, q2 // 2)])

        def a_pv(p, c, n):
            ring, rk = RING[p]
            bank = 4 + 2 * p + (n // 2) % 2
            js = [j for j in (n - 1, n, n + 1) if 0 <= j < 16]
            for lhs_kind, cbase in (("v", 0), ("ones", 256)):
                c0 = cbase + (n % 2) * 128
                for h2 in range(2):
                    orow = slice(h2 * 64, h2 * 64 + 64)
                    for idx, j in enumerate(js):
                        qs_j = max(j - 1, 0) * 128
                        col = n * 128 - qs_j
                        lhs = V[:, j, h2 * 64:(h2 + 1) * 64] if lhs_kind == "v" else ONES[:, 0:64]
                        mm(PS[orow, bank, c0:c0 + 128], lhs, ring[:, j % 4, h2, col:col + 128],
                           idx == 0, idx == len(js) - 1,
                           reads=[("V", j), ("ONES",), (rk, j % 4), RG_QR, RG_TT, RG_PTB], writes=[("PS", bank)])
            if n % 2 == 1:
                ep_pending.append((lambda bank, c, q2: lambda: epilogue_a(bank, c, q2))(bank, c, n // 2))

        def flush_ep():
            while ep_pending:
                ep_pending.pop(0)()

        for cs in ((0, 1), (2, 3)):
            cur = [a_scores(p, c, 0) for p, c in enumerate(cs)]
            for j in range(16):
                for p, c in enumerate(cs):
                    a_exp(p, c, j, cur[p][0], cur[p][1])
                flush_ep()
                if j >= 2:
                    for p, c in enumerate(cs):
                        a_pv(p, c, j - 2)
                if j + 1 < 16:
                    cur = [a_scores(p, c, j + 1) for p, c in enumerate(cs)]
            for n in (14, 15):
                for p, c in enumerate(cs):
                    a_pv(p, c, n)
        flush_ep()
        switch(RG_PTB)

        def b_scores(c, Q, j):
            sbk = 2 * (step[0] % 2)
            step[0] += 1
            for h2 in range(2):
                rows = slice(h2 * 64, h2 * 64 + 64)
                mm(PS[:, sbk + h2, :], KB[rows, tsl(j, 128)], QB[rows, c, tsl(Q)], True, True,
                   reads=[("KB", j // 4), ("QB", c, Q), RG_QR], writes=[("PS", sbk + h2)])
            return sbk

        bstep = [0]

        def b_exp(c, Q, j, sbk):
            slot = bstep[0] % 3
            bstep[0] += 1
            act(PTB[:, slot, :, :], PS[:, sbk:sbk + 2, :], AF.Exp, reads=[("PS", sbk), ("PS", sbk + 1), RG_TT, RG_PTB],
                writes=[("PTB", slot)], scale=0.125)
            return slot

        def b_pv(c, Q, j, slot):
            it = c * 4 + Q
            ob, db = 4 + it % 2, 6 + it % 2
            for h2 in range(2):
                orow = slice(h2 * 64, h2 * 64 + 64)
                mm(PS[orow, ob, :], V[:, j, 128 + h2 * 64:128 + (h2 + 1) * 64], PTB[:, slot, h2, :], j == 0, j == 15,
                   reads=[("V", j), ("PTB", slot), RG_QR, RG_TT], writes=[("PS", ob)])
            for h2 in range(2):
                orow = slice(h2 * 64, h2 * 64 + 64)
                mm(PS[orow, db, :], ONES[:, 0:64], PTB[:, slot, h2, :], j == 0, j == 15,
                   reads=[("ONES",), ("PTB", slot), RG_TT], writes=[("PS", db)])
            if j == 15:
                epilogue(ob, db, 1, c, Q, False)

        seq = [(c, Q, j) for c in range(4) for Q in range(4) for j in range(16)]
        pend = b_scores(*seq[0])
        prev = None
        for i, (c, Q, j) in enumerate(seq):
            cur = pend
            if i + 1 < len(seq):
                pend = b_scores(*seq[i + 1])
            slot = b_exp(c, Q, j, cur)
            if prev is not None:
                b_pv(*prev)
            prev = (c, Q, j, slot)
        b_pv(*prev)

        if dbg == 2:
            dump("dSZ", SZ[:].rearrange("p c t -> p (c t)"), [128, 8 * SEQ], BF16)
            s.emit(es)
            return nc
        switch(RG_TT)
        switch(RG_QR)
        p3 = [0]

        def p3_unit(slot, g, oo, n, inst):
            o = 2 * (g - 5) + oo
            b0 = 4 * inst
            for kc in range(8):
                mm(PS[:, b0, :], WROT[:, slot, kc, oo * 128:(oo + 1) * 128], XT[:, kc, tsl(n)],
                   kc == 0, kc == 7, reads=[("W", slot, kc // 4), ("XT", n, kc // 4)], writes=[("PS", b0)])
            for kc in range(8):
                mm(PS[:, b0 + 1, :], WROT[:, slot, kc, 256 + oo * 128:256 + (oo + 1) * 128], XT[:, kc, tsl(n)],
                   kc == 0, kc == 7, reads=[("W", slot, kc // 4), ("XT", n, kc // 4)], writes=[("PS", b0 + 1)])
            for kc in range(4):
                mm(PS[:, b0 + 2, :], WPA[:, kc, tsl(o, 128)], SZ[:, kc, tsl(n)], kc == 0, kc == 3,
                   reads=[("WPA",), ("SZ", kc, n), RG_CW], writes=[("PS", b0 + 2)])
            for kc in range(4):
                mm(PS[:, b0 + 3, :], WPB[:, kc, tsl(o, 128)], SZ[:, 4 + kc, tsl(n)], kc == 0, kc == 3,
                   reads=[("WPB",), ("SZ", 4 + kc, n), RG_CW], writes=[("PS", b0 + 3)])
            g0, g1, t2, t3 = (G3[4 * inst + i] for i in range(4))
            kk = [("G3", 4 * inst + i) for i in range(4)]
            act(g0, PS[:, b0, :], AF.Sigmoid, reads=[("PS", b0), ("CV",), RG_TT], writes=[kk[0]],
                bias=CV[:, o:o + 1], scale=1.0)
            act(g1, PS[:, b0 + 1, :], AF.Sigmoid, reads=[("PS", b0 + 1), ("CV",), RG_TT], writes=[kk[1]],
                bias=CV[:, 8 + o:9 + o], scale=1.0)
            tt("dve", t2, PS[:, b0 + 2, :], g0, ALU.mult, reads=[("PS", b0 + 2), kk[0], RG_TT], writes=[kk[2]])
            tt("dve", t3, PS[:, b0 + 3, :], g1, ALU.mult, reads=[("PS", b0 + 3), kk[1], RG_TT], writes=[kk[3]])
            tt("pool", MT[:, o, tsl(n)], t2, t3, ALU.add, reads=[kk[2], kk[3], RG_QR, RG_TT], writes=[("MT", o, n)])

        for pos in range(5, 8):
            g = WSEQ[pos]
            slot = pos % 2
            load_w(WSEQ[pos + 1], (pos + 1) % 2)
            for oo in range(2):
                for n in range(4):
                    inst = p3[0] % 2
                    p3[0] += 1
                    p3_unit(slot, g, oo, n, inst)

        def load_x(t):
            inst = t % 2
            s.op("sp", lambda e: e.dma_start(out=X4[inst], in_=d_x[t * 128:(t + 1) * 128, :]),
                 reads=[RG_TT], writes=[("X4", inst)], dma=True)

        load_x(0)
        load_x(1)

        def p4_front(t):
            inst = t % 2
            yb0 = 4 + (t % 2) * 2
            for h in range(2):
                for kc in range(8):
                    mm(PS[:, yb0 + h, :], MT[:, kc, tsl(t, 128)], WO[:, kc, tsl(h)], kc == 0, kc == 7,
                       reads=[("MT", kc, t // 4), ("WO",), RG_CW, RG_QR], writes=[("PS", yb0 + h)])
            r4, x4 = R4[t % 3], X4[inst]
            kr, kx = ("R4", t % 3), ("X4", inst)
            r4v = r4.rearrange("p (h f) -> p h f", h=2)
            x4v = x4.rearrange("p (h f) -> p h f", h=2)
            s.op("dve", lambda e: e.scalar_tensor_tensor(
                out=r4v, in0=x4v, scalar=DN_ALPHA, in1=PS[:, yb0:yb0 + 2, :], op0=ALU.mult, op1=ALU.add),
                reads=[kx, ("PS", yb0), ("PS", yb0 + 1), RG_TT, RG_QR], writes=[kr])
            if t + 2 < 16:
                load_x(t + 2)
            for h in range(2):
                s.op("dve", (lambda h: lambda e: e.bn_stats(out=STT[:, inst, 6 * h:6 * h + 6], in_=r4[:, tsl(h)]))(h),
                     reads=[kr], writes=[("STT", inst, h)])
            s.op("dve", lambda e: e.bn_aggr(out=MV[:, inst, :], in_=STT[:, inst, :]),
                 reads=[("STT", inst, 0), ("STT", inst, 1)], writes=[("MV", inst)])
        def p4_front_b(t):
            inst = t % 2
            r4, kr = R4[t % 3], ("R4", t % 3)
            tt("pool", SM2[:, inst, 0:1], MV[:, inst, 1:2], CONSTP[:, 0:1], ALU.add, reads=[("MV", inst), ("CONSTP",)],
               writes=[("SM2", inst)])
            tt("pool", SM[:, inst, 0:1], SM2[:, inst, 0:1], CONSTP[:, 1:2], ALU.pow, reads=[("SM2", inst), ("CONSTP",)],
               writes=[("SM0", inst)])
            s.op("dve", lambda e: e.scalar_tensor_tensor(
                out=SM[:, inst, 1:2], in0=MV[:, inst, 0:1], scalar=-1.0, in1=SM[:, inst, 0:1],
                op0=ALU.mult, op1=ALU.mult),
                reads=[("MV", inst), ("SM0", inst)], writes=[("SM1", inst)])
            act(r4, r4, AF.Identity, reads=[kr, ("SM0", inst), ("SM1", inst)], writes=[kr],
                bias=SM[:, inst, 1:2], scale=SM[:, inst, 0:1])

        def p4_back(t):
            r4, o4 = R4[t % 3], O4[0]
            kr, ko = ("R4", t % 3), ("O4", 0)
            tt("pool", o4, r4, LNG[:], ALU.mult, reads=[kr, ("LNG",), RG_QR], writes=[ko])
            tt("dve", r4[:, 0:640], o4[:, 0:640], LNB[:, 0:640], ALU.add, reads=[ko, ("LNB",), RG_QR], writes=[kr])
            tt("pool", r4[:, 640:1024], o4[:, 640:1024], LNB[:, 640:1024], ALU.add, reads=[ko, ("LNB",), RG_QR],
               writes=[("R4b", t % 3)])
            s.op("sp", lambda e: e.dma_start(out=d_out[t * 128:(t + 1) * 128, :], in_=r4),
                 reads=[kr, ("R4b", t % 3), RG_QR], dma=True)

        def p4_step(t):
            p4_front(t)
            if t >= 1:
                p4_back(t - 1)
            p4_front_b(t)

        g = WSEQ[8]
        slot = 8 % 2
        p3_unit(slot, g, 0, 0, 0)
        p3_unit(slot, g, 1, 0, 0)
        for n in range(1, 4):
            t0 = 4 * (n - 1)
            p3_unit(slot, g, 0, n, 0)
            p4_step(t0)
            p4_step(t0 + 1)
            p3_unit(slot, g, 1, n, 0)
            p4_step(t0 + 2)
            p4_step(t0 + 3)
        for t in range(12, 16):
            p4_step(t)
        p4_back(15)

        s.emit(es)
    return nc


def _pair_idx(base):
    idx = []
    for c in range(4):
        idx += list(range(base + 64 * c, base + 64 * c + 64))
        idx += list(range(base + 64 * (4 + c), base + 64 * (4 + c) + 64))
    return idx


def _win_cols():
    QA_, KA_, VA_, ZA_, QB_, KB_, VB_, ZB_, GA_, GB_ = 0, 512, 640, 768, 1280, 1792, 1920, 2048, 2560, 3584
    cols = []
    cols += _pair_idx(QB_)
    cols += list(range(KB_, KB_ + 128)) + list(range(KA_, KA_ + 128)) + list(range(VA_, VA_ + 128)) + list(
        range(VB_, VB_ + 128))
    cols += _pair_idx(QA_)
    cols += _pair_idx(ZA_)
    cols += _pair_idx(ZB_)
    for g in range(4):
        for base in (GA_, GB_):
            for oo in range(2):
                o = 2 * g + oo
                cols += list(range(base + o * 128, base + (o + 1) * 128))
    return np.array(cols, dtype=np.int64)


def _const_tables():
    p = np.arange(128)
    d = p % 64
    i = d % 32
    t = np.arange(SEQ, dtype=np.float64)
    freqs = 10000.0 ** (-np.arange(16, dtype=np.float64) / 16.0)
    f = np.where(i < 16, freqs[i % 16], freqs[(i - 16) % 16])
    pos = np.where((i < 16)[:, None], np.floor(t / 64.0)[None, :], np.mod(t, 64.0)[None, :])
    ang = pos * f[:, None]
    cos = np.cos(ang)
    sin = np.sin(ang) * np.where(d < 32, -1.0, 1.0)[:, None]
    sk = np.arange(128)[:, None]
    qq = np.arange(384)[None, :]
    dist = np.abs(qq - 128 - sk).astype(np.float64)
    bias = np.zeros((128, 8, 384), np.float64)
    for c in range(4):
        for h2 in range(2):
            h = c + 4 * h2
            slope = 2.0 ** (-8.0 * (h + 1) / 8.0)
            bias[:, 2 * c + h2, :] = np.where(dist <= 128, np.exp(-slope * dist), 0.0)
    ident = np.eye(128)
    bones = (p[:, None] // 64 == p[None, :] // 64).astype(np.float64) / 64.0
    sw = np.where(d < 32, p + 32, p - 32)
    perm = np.zeros((128, 128))
    perm[p, sw] = 1.0
    mats = np.concatenate([ident, bones, perm], axis=1)
    f32 = lambda a: np.ascontiguousarray(a, dtype=np.float32)
    return f32(cos), f32(sin), f32(bias.reshape(128, 8 * 384)), f32(mats)


_CACHE = {}


def kernel(x, w_in, b_gate, sink_a, qnorm_b, knorm_b, w_proj_a, w_proj_b, w_out, ln_g, ln_b):
    x = np.asarray(x, dtype=np.float32)
    w = np.asarray(w_in, dtype=np.float32)[0]
    win = np.ascontiguousarray(w[:, _win_cols()])
    rows = np.array(_pair_idx(0), dtype=np.int64)
    wpa = np.ascontiguousarray(np.asarray(w_proj_a, np.float32)[0][rows])
    wpb = np.ascontiguousarray(np.asarray(w_proj_b, np.float32)[0][rows])
    wo = np.ascontiguousarray(np.asarray(w_out, np.float32)[0])
    cvec = np.zeros((128, 32), np.float32)
    cvec[:, 0:16] = np.asarray(b_gate, np.float32)[0].reshape(16, 128).T
    sk = np.asarray(sink_a, np.float32)[0]
    for c in range(4):
        cvec[:64, 16 + c] = sk[c]
        cvec[64:, 16 + c] = sk[4 + c]
    cvec[:, 20] = np.tile(np.asarray(qnorm_b, np.float32)[0], 2)
    cvec[:, 21] = np.tile(np.asarray(knorm_b, np.float32)[0], 2)
    lng = np.ascontiguousarray(np.broadcast_to(np.asarray(ln_g, np.float32)[0][None, :], (128, DM)))
    lnb = np.ascontiguousarray(np.broadcast_to(np.asarray(ln_b, np.float32)[0][None, :], (128, DM)))
    cos, sin, bias, mats = _const_tables()
    if "nc" not in _CACHE:
        _CACHE["nc"] = build_program()
    nc = _CACHE["nc"]
    shared = {"win": win, "wpa": wpa, "wpb": wpb, "wo": wo, "cvec": cvec, "lng": lng, "lnb": lnb,
              "cos": cos, "sin": sin, "biasA": bias, "mats": mats}
    in_maps = []
    for b in range(8):
        m = dict(shared)
        m["xT"] = np.ascontiguousarray(x[b].T)
        m["x"] = np.ascontiguousarray(x[b])
        in_maps.append(m)
    res = run_bass_kernel_spmd(nc, in_maps, core_ids=list(range(8)))
    return np.stack([np.asarray(r["out"], dtype=np.float32) for r in res.results], axis=0)
```

```python
import contextlib

import numpy as np

import concourse.bass as bass
import concourse.mybir as mybir
from concourse.bass_utils import run_bass_kernel_spmd

F32 = mybir.dt.float32
BF16 = mybir.dt.bfloat16
F32R = mybir.dt.float32r
AF = mybir.ActivationFunctionType
ALU = mybir.AluOpType

SEQ = 2048
DM = 1024
QK_EPS = 1e-6
LN_EPS = 1e-5
DN_ALPHA = 2.0 ** 0.25
MASK_NEG = -4000.0
DBGF = {}

ENGS = ("pe", "act", "dve", "pool", "sp")
DMA_POOL = {"sp": 12, "pool": 12, "act": 4}


class Op:
    __slots__ = ("idx", "eng", "fn", "dma", "deps", "signal", "cnt", "pos", "dsem", "dval")

    def __init__(self, idx, eng, fn, dma):
        self.idx = idx
        self.eng = eng
        self.fn = fn
        self.dma = dma
        self.deps = ()
        self.signal = False
        self.cnt = 0
        self.pos = 0
        self.dsem = None
        self.dval = 0


class Sched:
    def __init__(self, nc):
        self.nc = nc
        self.ops = []
        self.last_w = {}
        self.rd_eng = {}
        self.rd_dma = {}

    def op(self, eng, fn, reads=(), writes=(), dma=False):
        o = Op(len(self.ops), eng, fn, dma)
        deps = set()
        for r in reads:
            w = self.last_w.get(r)
            if w is not None:
                deps.add(w)
        for k in writes:
            w = self.last_w.get(k)
            if w is not None:
                deps.add(w)
            deps.update(self.rd_eng.get(k, {}).values())
            deps.update(self.rd_dma.get(k, ()))
        o.deps = tuple(sorted(deps))
        for r in reads:
            if dma:
                self.rd_dma.setdefault(r, []).append(o.idx)
            else:
                self.rd_eng.setdefault(r, {})[eng] = o.idx
        for k in writes:
            self.last_w[k] = o.idx
            self.rd_eng[k] = {}
            self.rd_dma[k] = []
        self.ops.append(o)
        return o

    def _need_sync(self, p, o):
        if p.eng == o.eng and not o.dma:
            if p.eng == "pe":
                return False
            if o.pos - p.pos > 2:
                return False
        return True

    def emit(self, es):
        nc = self.nc
        ops = self.ops
        streams = {e: [] for e in ENGS}
        for o in ops:
            o.pos = len(streams[o.eng])
            streams[o.eng].append(o)
        for o in ops:
            for d in o.deps:
                p = ops[d]
                if p.dma:
                    continue
                if self._need_sync(p, o):
                    p.signal = True
        for e in ENGS:
            c = 0
            for o in streams[e]:
                if o.dma:
                    continue
                if o.signal:
                    c += 1
                o.cnt = c
        esem = {e: es.enter_context(nc.semaphore(f"es_{e}")) for e in ("pe", "act", "dve", "pool")}
        dsems = {q: [es.enter_context(nc.semaphore(f"ds_{q}{i}")) for i in range(n)]
                 for q, n in DMA_POOL.items()}
        for q, n in DMA_POOL.items():
            k = 0
            for o in streams[q]:
                if o.dma:
                    o.dsem = (q, k % n)
                    o.dval = 16 * (k // n + 1)
                    k += 1

        def run_stream(e, engobj):
            waited = {}

            def wait(semkey, sem, val):
                if waited.get(semkey, 0) >= val:
                    return
                waited[semkey] = val
                engobj.wait_ge(sem, val)

            for o in streams[e]:
                for d in o.deps:
                    p = ops[d]
                    if p.dma:
                        wait(("d",) + p.dsem, dsems[p.dsem[0]][p.dsem[1]], p.dval)
                    elif self._need_sync(p, o):
                        wait(("e", p.eng), esem[p.eng], p.cnt)
                if o.dma:
                    sem = dsems[o.dsem[0]][o.dsem[1]]
                    if o.dval > 16:
                        wait(("d",) + o.dsem, sem, o.dval - 16)
                    o.fn(engobj).then_inc(sem, 16)
                else:
                    ins = o.fn(engobj)
                    if o.signal:
                        ins.then_inc(esem[e], 1)
            if e in DMA_POOL:
                last = {}
                for o in streams[e]:
                    if o.dma:
                        last[o.dsem] = o.dval
                for k, v in last.items():
                    wait(("d",) + k, dsems[k[0]][k[1]], v)

        block = es.enter_context(nc.Block())

        @block.sync
        def _(eng):
            run_stream("sp", eng)

        @block.tensor
        def _(eng):
            run_stream("pe", eng)

        @block.scalar
        def _(eng):
            run_stream("act", eng)

        @block.vector
        def _(eng):
            run_stream("dve", eng)

        @block.gpsimd
        def _(eng):
            run_stream("pool", eng)


def build_program(dbg=None):
    nc = bass.Bass("TRN2", target_bir_lowering=False)
    d_xT = nc.dram_tensor("xT", [DM, SEQ], F32, kind="ExternalInput").ap()
    d_x = nc.dram_tensor("x", [SEQ, DM], F32, kind="ExternalInput").ap()
    d_win = nc.dram_tensor("win", [DM, 4608], F32, kind="ExternalInput").ap()
    d_wpa = nc.dram_tensor("wpa", [512, DM], F32, kind="ExternalInput").ap()
    d_wpb = nc.dram_tensor("wpb", [512, DM], F32, kind="ExternalInput").ap()
    d_wo = nc.dram_tensor("wo", [DM, DM], F32, kind="ExternalInput").ap()
    d_cvec = nc.dram_tensor("cvec", [128, 32], F32, kind="ExternalInput").ap()
    d_lng = nc.dram_tensor("lng", [128, DM], F32, kind="ExternalInput").ap()
    d_lnb = nc.dram_tensor("lnb", [128, DM], F32, kind="ExternalInput").ap()
    d_cos = nc.dram_tensor("cos", [128, SEQ], F32, kind="ExternalInput").ap()
    d_sin = nc.dram_tensor("sin", [128, SEQ], F32, kind="ExternalInput").ap()
    d_bias = nc.dram_tensor("biasA", [128, 8 * 384], F32, kind="ExternalInput").ap()
    d_mats = nc.dram_tensor("mats", [128, 384], F32, kind="ExternalInput").ap()
    d_out = nc.dram_tensor("out", [SEQ, DM], F32, kind="ExternalOutput").ap()

    d_win_v = d_win.rearrange("(kc p) c -> p kc c", p=128)
    d_xT_v = d_xT.rearrange("(kc p) t -> p kc t", p=128)

    with contextlib.ExitStack() as es:
        def sb(name, shape, dt):
            return es.enter_context(nc.sbuf_tensor(name, shape, dt))

        XT = sb("XT", [128, 8, SEQ], BF16)
        SZ = sb("SZ", [128, 8, SEQ], BF16)
        QR = sb("QR", [128, 24576], BF16)
        WROT = sb("WROT", [128, 2, 8, 512], BF16)
        CW = sb("CW", [128, 8192], F32)
        TT = sb("TT", [128, 6144], F32)
        LNG = sb("LNG", [128, DM], F32)
        LNB = sb("LNB", [128, DM], F32)
        CV = sb("CV", [128, 32], F32)
        ESINK = sb("ESINK", [128, 4], F32)
        MATS = sb("MATS", [128, 256], F32)
        ONES = sb("ONES", [128, 64], BF16)
        STT = sb("STT", [128, 2, 12], F32)
        MV = sb("MV", [128, 2, 2], F32)
        SM = sb("SM", [128, 2, 2], F32)
        SCR = sb("SCR", [128, 8], F32)
        TSQ = sb("TSQ", [128, 2, 512], F32)
        PS = es.enter_context(nc.psum_tensor("PS", [128, 8, 512], F32))

        QA = QR[:, 0:8192].rearrange("p (c t) -> p c t", c=4)
        QB = QR[:, 8192:16384].rearrange("p (c t) -> p c t", c=4)
        KA = QR[:, 16384:18432]
        KB = QR[:, 18432:20480]
        V = QR[:, 20480:24576].rearrange("p (t f) -> p t f", t=16)
        MT = QR[:, 0:16384].rearrange("p (c t) -> p c t", c=8)
        COS = CW[:, 0:2048]
        SINS = CW[:, 2048:4096]
        T1 = [CW[:, 4096 + 512 * i: 4096 + 512 * (i + 1)] for i in range(8)]
        WPA = CW[:, 0:2048].bitcast(BF16).rearrange("p (c o) -> p c o", c=4)
        WPB = CW[:, 2048:4096].bitcast(BF16).rearrange("p (c o) -> p c o", c=4)
        WO = CW[:, 4096:8192].bitcast(BF16).rearrange("p (c o) -> p c o", c=8)
        BIAS = TT[:, 0:1536].bitcast(BF16).rearrange("p (h q) -> p h q", h=8)
        PTA = TT[:, 1536:3072].bitcast(BF16).rearrange("p (s h q) -> p s h q", s=4, h=2)
        PTB = TT[:, 3072:4608].bitcast(BF16).rearrange("p (s h q) -> p s h q", s=3, h=2)
        E2 = [TT[:, 4608 + 512 * i: 4608 + 512 * (i + 1)] for i in range(2)]
        PTA2 = TT[:, 3072:4608].bitcast(BF16).rearrange("p (s h q) -> p s h q", s=4, h=2)
        G3 = [TT[:, 512 * i: 512 * (i + 1)] for i in range(8)]
        X4 = [TT[:, 4096:5120], TT[:, 5120:6144]]
        QRT = QR[:, 16384:24576].bitcast(F32)
        R4 = [QRT[:, 1024 * i: 1024 * (i + 1)] for i in range(3)]
        O4 = [QRT[:, 3072:4096]]
        MATSR = sb("MATSR", [128, 256], F32)
        BONES = MATSR[:, 0:128]
        PERM = MATSR[:, 128:256]

        s = Sched(nc)
        RG_CW, RG_TT, RG_QR = ("RG", "CW"), ("RG", "TT"), ("RG", "QR")
        RG_PTB = ("RG", "PTB")

        def tsl(n, w=512):
            return slice(n * w, (n + 1) * w)

        def mm(out, lhsT, rhs, start, stop, reads, writes):
            s.op("pe", lambda e: e.matmul(out, lhsT=lhsT, rhs=rhs, start=start, stop=stop),
                 reads=reads, writes=writes)

        def act(out, in_, func, reads, writes, bias=None, scale=None):
            kw = {}
            if bias is not None:
                kw["bias"] = bias
            if scale is not None:
                kw["scale"] = scale
            s.op("act", lambda e: e.activation(out=out, in_=in_, func=func, **kw), reads=reads, writes=writes)

        def tt(eng, out, in0, in1, op, reads, writes):
            s.op(eng, lambda e: e.tensor_tensor(out=out, in0=in0, in1=in1, op=op), reads=reads, writes=writes)

        sw_n = [0]

        def switch(rg):
            k = sw_n[0]
            sw_n[0] += 1
            s.op("dve", lambda e: e.memset(SCR[:, k:k + 1], 0.0), writes=[rg])

        s.op("sp", lambda e: e.dma_start(out=CV[:], in_=d_cvec), writes=[("CV",)], dma=True)
        s.op("sp", lambda e: e.dma_start(out=MATS[:], in_=d_mats[:, 128:384]), writes=[("MATS0",)], dma=True)
        s.op("dve", lambda e: e.tensor_copy(out=MATSR[:], in_=MATS[:]), reads=[("MATS0",)], writes=[("MATS",)])
        def load_xt(n):
            for hf in range(2):
                s.op("pool", (lambda hf: lambda e: e.dma_start(out=XT[:, 4 * hf:4 * hf + 4, n * 512:(n + 1) * 512],
                                                          in_=d_xT_v[:, 4 * hf:4 * hf + 4, n * 512:(n + 1) * 512]))(hf),
                     writes=[("XT", n, hf)], dma=True)

        def load_w(g, slot):
            for hf in range(2):
                s.op("pool", (lambda hf: lambda e: e.dma_start(out=WROT[:, slot, 4 * hf:4 * hf + 4, :],
                                                          in_=d_win_v[:, 4 * hf:4 * hf + 4, g * 512:(g + 1) * 512]))(hf),
                     writes=[("W", slot, hf)], dma=True)

        WSEQ = [3, 4, 2, 1, 0, 5, 6, 7, 8]
        load_w(WSEQ[0], 0)
        load_xt(0)
        load_xt(1)
        load_w(WSEQ[1], 1)
        load_xt(2)
        load_xt(3)
        s.op("sp", lambda e: e.dma_start(out=COS, in_=d_cos), reads=[RG_CW], writes=[("COS",)], dma=True)
        s.op("sp", lambda e: e.dma_start(out=SINS, in_=d_sin), reads=[RG_CW], writes=[("SIN",)], dma=True)
        s.op("sp", lambda e: e.dma_start(out=LNG[:], in_=d_lng), writes=[("LNG",)], dma=True)
        s.op("sp", lambda e: e.dma_start(out=LNB[:], in_=d_lnb), writes=[("LNB",)], dma=True)
        BIASF = TT[:, 0:1536].bitcast(BF16)
        for hf in range(2):
            s.op("pool", (lambda hf: lambda e: e.dma_start(out=BIASF[:, hf * 1536:(hf + 1) * 1536],
                                                          in_=d_bias[:, hf * 1536:(hf + 1) * 1536]))(hf),
                 reads=[RG_TT], writes=[("BIAS", hf)], dma=True)
        s.op("pool", lambda e: e.memset(ONES[:], 1.0), writes=[("ONES",)])
        act(ESINK[:], CV[:, 16:20], AF.Exp, reads=[("CV",)], writes=[("ESINK",)])

        bank_rr = [0]
        pipe_rr = [0]

        def next_bank():
            b = bank_rr[0] % 4
            bank_rr[0] += 1
            return b

        def evac_rms(bank, n, gcol, dst, dst_key):
            inst = pipe_rr[0] % 2
            pipe_rr[0] += 1
            tsq, tu, tb = TSQ[:, inst, :], T1[3 * inst + 1], T1[3 * inst + 2]
            ksq, ku, kb_ = ("TSQ", inst), ("T1", 3 * inst + 1), ("T1", 3 * inst + 2)
            bm, br = 4 + inst, 6 + inst
            pb = PS[:, bank, :]
            act(tsq, pb, AF.Square, reads=[("PS", bank), RG_CW], writes=[ksq])
            s.op("dve", lambda e: e.tensor_scalar_mul(out=tu, in0=pb, scalar1=CV[:, gcol:gcol + 1]),
                 reads=[("PS", bank), ("CV",), RG_CW, ksq], writes=[ku])
            tc_, kc_ = T1[6 + inst], ("T1", 6 + inst)
            tt("pool", tc_, tu, COS[:, tsl(n)], ALU.mult, reads=[ku, ("COS",), RG_CW], writes=[kc_])

            def part2():
                mm(PS[:, bm, :], BONES, tsq, True, True, reads=[("MATS",), ksq], writes=[("PS", bm)])
                mm(PS[:, br, :], PERM, tu, True, True, reads=[("MATS",), ku], writes=[("PS", br)])
                act(tsq, PS[:, bm, :], AF.Ln, reads=[("PS", bm), RG_CW], writes=[ksq], bias=QK_EPS, scale=1.0)
                act(tsq, tsq, AF.Exp, reads=[ksq], writes=[ksq], scale=-0.5)
                tt("dve", tb, PS[:, br, :], SINS[:, tsl(n)], ALU.mult, reads=[("PS", br), ("SIN",), RG_CW], writes=[kb_])
                tt("dve", tc_, tc_, tb, ALU.add, reads=[kc_, kb_, RG_CW], writes=[kc_])
                tt("pool", dst, tc_, tsq, ALU.mult, reads=[kc_, ksq, RG_QR, RG_CW], writes=[dst_key])

            rms_pending.append(part2)

        rms_pending = []

        def flush_rms(keep):
            while len(rms_pending) > keep:
                rms_pending.pop(0)()

        def inproj_chunk(slot, f, n, kind, c):
            bank = next_bank()
            for kc in range(8):
                mm(PS[:, bank, :], WROT[:, slot, kc, f * 128:(f + 1) * 128], XT[:, kc, tsl(n)],
                   kc == 0, kc == 7, reads=[("W", slot, kc // 4), ("XT", n, kc // 4)], writes=[("PS", bank)])
            flush_rms(0)
            pb = PS[:, bank, :]
            if kind == "qa":
                dst = QA[:, c, tsl(n)]
                s.op("dve", lambda e: e.tensor_copy(out=dst, in_=pb), reads=[("PS", bank), RG_QR],
                     writes=[("QA", c, n)])
            elif kind == "ka":
                dst = KA[:, tsl(n)]
                s.op("dve", lambda e: e.tensor_copy(out=dst, in_=pb), reads=[("PS", bank), RG_QR],
                     writes=[("KA", n)])
            elif kind in ("za", "zb"):
                m = 0 if kind == "za" else 1
                act(SZ[:, m * 4 + c, tsl(n)], pb, AF.Silu, reads=[("PS", bank)], writes=[("SZ", m * 4 + c, n)])
            elif kind == "qb":
                if DBGF.get("norms"):
                    dst = QB[:, c, tsl(n)]
                    s.op("dve", lambda e: e.tensor_copy(out=dst, in_=pb), reads=[("PS", bank), RG_QR],
                         writes=[("QB", c, n)])
                else:
                    evac_rms(bank, n, 20, QB[:, c, tsl(n)], ("QB", c, n))
            elif kind == "kb":
                if DBGF.get("norms"):
                    dst = KB[:, tsl(n)]
                    s.op("dve", lambda e: e.tensor_copy(out=dst, in_=pb), reads=[("PS", bank), RG_QR],
                         writes=[("KB", n)])
                else:
                    evac_rms(bank, n, 21, KB[:, tsl(n)], ("KB", n))

        group_kinds = {
            0: [("qb", 0), ("qb", 1), ("qb", 2), ("qb", 3)],
            1: [("kb", 0), ("ka", 0), None, None],
            2: [("qa", 0), ("qa", 1), ("qa", 2), ("qa", 3)],
            3: [("za", 0), ("za", 1), ("za", 2), ("za", 3)],
            4: [("zb", 0), ("zb", 1), ("zb", 2), ("zb", 3)],
        }
        for pos in range(5):
            g = WSEQ[pos]
            slot = pos % 2
            if pos >= 1:
                load_w(WSEQ[pos + 1], (pos + 1) % 2)
            for n in range(4):
                for f in range(4):
                    kd = group_kinds[g][f]
                    if kd is None:
                        continue
                    inproj_chunk(slot, f, n, kd[0], kd[1])
            if g == 1:
                flush_rms(0)
                for t in range(16):
                    bank = next_bank()
                    for kc in range(8):
                        mm(PS[:, bank, 0:256], XT[:, kc, tsl(t, 128)], WROT[:, slot, kc, 256:512],
                           kc == 0, kc == 7, reads=[("W", slot, kc // 4), ("XT", t // 4, kc // 4)], writes=[("PS", bank)])
                    s.op("dve", (lambda t, bank: lambda e: e.tensor_copy(out=V[:, t, :], in_=PS[:, bank, 0:256]))(t, bank),
                         reads=[("PS", bank), RG_QR], writes=[("V", t)])

        flush_rms(0)

        def dump(name, ap, shape, dt):
            d = nc.dram_tensor(name, shape, dt, kind="ExternalOutput").ap()
            s.op("sp", lambda e: e.dma_start(out=d, in_=ap), reads=list(s.last_w.keys()), dma=True)

        if dbg == 1:
            dump("dQR", QR[:], [128, 24576], BF16)
            dump("dSZ", SZ[:].rearrange("p c t -> p (c t)"), [128, 8 * SEQ], BF16)
            s.emit(es)
            return nc
        switch(RG_CW)
        s.op("pool", lambda e: e.dma_start(out=WPA, in_=d_wpa.rearrange("(kc p) o -> p kc o", p=128)),
             reads=[RG_CW], writes=[("WPA",)], dma=True)
        s.op("pool", lambda e: e.dma_start(out=WPB, in_=d_wpb.rearrange("(kc p) o -> p kc o", p=128)),
             reads=[RG_CW], writes=[("WPB",)], dma=True)
        s.op("pool", lambda e: e.dma_start(out=WO, in_=d_wo.rearrange("(kc p) o -> p kc o", p=128)),
             reads=[RG_CW], writes=[("WO",)], dma=True)

        ep_rr = [0]

        def epilogue(ob, db, m, c, Q, sink):
            e0, e1 = E2[0], E2[1]
            k0 = [("E2", 0, 0), ("E2", 0, 1)]
            k1 = [("E2", 1, 0), ("E2", 1, 1)]
            s.op("dve", lambda e: e.reciprocal(out=e0, in_=PS[:, db, :]), reads=[("PS", db), RG_TT], writes=k0)
            tt("dve", e1, PS[:, ob, :], e0, ALU.mult, reads=[("PS", ob), RG_TT] + k0, writes=k1)
            dst = SZ[:, m * 4 + c, tsl(Q)]
            tt("dve", dst, e1, dst, ALU.mult, reads=k1 + [("SZ", m * 4 + c, Q), RG_TT], writes=[("SZ", m * 4 + c, Q)])

        step = [0]
        ep_pending = []
        ep2 = [0]
        RING = [(PTA, "PTA"), (PTA2, "PTA2")]

        def a_scores(p, c, j):
            qs = max(j - 1, 0) * 128
            qe = min(j + 2, 16) * 128
            N = qe - qs
            boff = qs - (j - 1) * 128
            for h2 in range(2):
                rows = slice(h2 * 64, h2 * 64 + 64)
                bk = 2 * p + h2
                mm(PS[:, bk, 0:N], KA[rows, tsl(j, 128)], QA[rows, c, qs:qe], True, True,
                   reads=[("KA", j // 4), RG_QR] + [("QA", c, n) for n in range(qs // 512, (qe - 1) // 512 + 1)],
                   writes=[("PS", bk)])
            return N, boff

        def a_exp(p, c, j, N, boff):
            ring, rk = RING[p]
            slot = j % 4
            pt = ring[:, slot, :, 0:N]
            act(pt, PS[:, 2 * p:2 * p + 2, 0:N], AF.Exp, reads=[("PS", 2 * p), ("PS", 2 * p + 1), RG_TT, RG_PTB],
                writes=[(rk, slot)], scale=0.125)
            tt("dve", pt, pt, BIAS[:, 2 * c:2 * c + 2, boff:boff + N], ALU.mult,
               reads=[(rk, slot), ("BIAS", c // 2), RG_TT, RG_PTB], writes=[(rk, slot)])

        def epilogue_a(bank, c, q2):
            k = ep2[0] % 2
            ep2[0] += 1
            e0 = E2[0][:, k * 256:(k + 1) * 256]
            e1 = E2[1][:, k * 256:(k + 1) * 256]
            k0, k1 = ("E2", 0, k), ("E2", 1, k)
            act(e0, PS[:, bank, 256:512], AF.Ln, reads=[("PS", bank), ("ESINK",), RG_TT], writes=[k0],
                bias=ESINK[:, c:c + 1], scale=1.0)
            act(e0, e0, AF.Exp, reads=[k0], writes=[k0], scale=-1.0)
            tt("dve", e1, PS[:, bank, 0:256], e0, ALU.mult, reads=[("PS", bank), k0, RG_TT], writes=[k1])
            dst = SZ[:, c, q2 * 256:(q2 + 1) * 256]
            tt("dve", dst, e1, dst, ALU.mult, reads=[k1, ("SZ", c, q2 // 2), RG_TT], writes=[("SZ", c, q2 // 2)])

        def a_pv(p, c, n):
            ring, rk = RING[p]
            bank = 4 + 2 * p + (n // 2) % 2
            js = [j for j in (n - 1, n, n + 1) if 0 <= j < 16]
            for lhs_kind, cbase in (("v", 0), ("ones", 256)):
                c0 = cbase + (n % 2) * 128
                for h2 in range(2):
                    orow = slice(h2 * 64, h2 * 64 + 64)
                    for idx, j in enumerate(js):
                        qs_j = max(j - 1, 0) * 128
                        col = n * 128 - qs_j
                        lhs = V[:, j, h2 * 64:(h2 + 1) * 64] if lhs_kind == "v" else ONES[:, 0:64]
                        mm(PS[orow, bank, c0:c0 + 128], lhs, ring[:, j % 4, h2, col:col + 128],
                           idx == 0, idx == len(js) - 1,
                           reads=[("V", j), ("ONES",), (rk, j % 4), RG_QR, RG_TT, RG_PTB], writes=[("PS", bank)])
            if n % 2 == 1:
                ep_pending.append((lambda bank, c, q2: lambda: epilogue_a(bank, c, q2))(bank, c, n // 2))

        def flush_ep(limit=None):
            k = 0
            while ep_pending and (limit is None or k < limit):
                ep_pending.pop(0)()
                k += 1

        for cs in ((0, 1), (2, 3)):
            cur = [a_scores(p, c, 0) for p, c in enumerate(cs)]
            for j in range(16):
                for p, c in enumerate(cs):
                    a_exp(p, c, j, cur[p][0], cur[p][1])
                flush_ep(2)
                if j >= 2:
                    for p, c in enumerate(cs):
                        a_pv(p, c, j - 2)
                if j + 1 < 16:
                    cur = [a_scores(p, c, j + 1) for p, c in enumerate(cs)]
            for n in (14, 15):
                for p, c in enumerate(cs):
                    a_pv(p, c, n)
        flush_ep()
        switch(RG_PTB)

        def b_scores(c, Q, j):
            sbk = 2 * (step[0] % 2)
            step[0] += 1
            for h2 in range(2):
                rows = slice(h2 * 64, h2 * 64 + 64)
                mm(PS[:, sbk + h2, :], KB[rows, tsl(j, 128)], QB[rows, c, tsl(Q)], True, True,
                   reads=[("KB", j // 4), ("QB", c, Q), RG_QR], writes=[("PS", sbk + h2)])
            return sbk

        bstep = [0]

        def b_exp(c, Q, j, sbk):
            slot = bstep[0] % 3
            bstep[0] += 1
            act(PTB[:, slot, :, :], PS[:, sbk:sbk + 2, :], AF.Exp, reads=[("PS", sbk), ("PS", sbk + 1), RG_TT, RG_PTB],
                writes=[("PTB", slot)], scale=0.125)
            return slot

        def b_pv(c, Q, j, slot):
            it = c * 4 + Q
            ob, db = 4 + it % 2, 6 + it % 2
            for h2 in range(2):
                orow = slice(h2 * 64, h2 * 64 + 64)
                mm(PS[orow, ob, :], V[:, j, 128 + h2 * 64:128 + (h2 + 1) * 64], PTB[:, slot, h2, :], j == 0, j == 15,
                   reads=[("V", j), ("PTB", slot), RG_QR, RG_TT], writes=[("PS", ob)])
            for h2 in range(2):
                orow = slice(h2 * 64, h2 * 64 + 64)
                mm(PS[orow, db, :], ONES[:, 0:64], PTB[:, slot, h2, :], j == 0, j == 15,
                   reads=[("ONES",), ("PTB", slot), RG_TT], writes=[("PS", db)])
            if j == 15:
                epilogue(ob, db, 1, c, Q, False)

        seq = [(c, Q, j) for c in range(4) for Q in range(4) for j in range(16)]
        pend = b_scores(*seq[0])
        prev = None
        for i, (c, Q, j) in enumerate(seq):
            cur = pend
            if i + 1 < len(seq):
                pend = b_scores(*seq[i + 1])
            slot = b_exp(c, Q, j, cur)
            if prev is not None:
                b_pv(*prev)
            prev = (c, Q, j, slot)
        b_pv(*prev)

        if dbg == 2:
            dump("dSZ", SZ[:].rearrange("p c t -> p (c t)"), [128, 8 * SEQ], BF16)
            s.emit(es)
            return nc
        switch(RG_TT)
        switch(RG_QR)
        p3 = [0]

        def p3_unit(slot, g, oo, n, inst):
            o = 2 * (g - 5) + oo
            b0 = 4 * inst
            for kc in range(8):
                mm(PS[:, b0, :], WROT[:, slot, kc, oo * 128:(oo + 1) * 128], XT[:, kc, tsl(n)],
                   kc == 0, kc == 7, reads=[("W", slot, kc // 4), ("XT", n, kc // 4)], writes=[("PS", b0)])
            for kc in range(8):
                mm(PS[:, b0 + 1, :], WROT[:, slot, kc, 256 + oo * 128:256 + (oo + 1) * 128], XT[:, kc, tsl(n)],
                   kc == 0, kc == 7, reads=[("W", slot, kc // 4), ("XT", n, kc // 4)], writes=[("PS", b0 + 1)])
            for kc in range(4):
                mm(PS[:, b0 + 2, :], WPA[:, kc, tsl(o, 128)], SZ[:, kc, tsl(n)], kc == 0, kc == 3,
                   reads=[("WPA",), ("SZ", kc, n), RG_CW], writes=[("PS", b0 + 2)])
            for kc in range(4):
                mm(PS[:, b0 + 3, :], WPB[:, kc, tsl(o, 128)], SZ[:, 4 + kc, tsl(n)], kc == 0, kc == 3,
                   reads=[("WPB",), ("SZ", 4 + kc, n), RG_CW], writes=[("PS", b0 + 3)])
            g0, g1, t2, t3 = (G3[4 * inst + i] for i in range(4))
            kk = [("G3", 4 * inst + i) for i in range(4)]
            act(g0, PS[:, b0, :], AF.Sigmoid, reads=[("PS", b0), ("CV",), RG_TT], writes=[kk[0]],
                bias=CV[:, o:o + 1], scale=1.0)
            act(g1, PS[:, b0 + 1, :], AF.Sigmoid, reads=[("PS", b0 + 1), ("CV",), RG_TT], writes=[kk[1]],
                bias=CV[:, 8 + o:9 + o], scale=1.0)
            tt("dve", t2, PS[:, b0 + 2, :], g0, ALU.mult, reads=[("PS", b0 + 2), kk[0], RG_TT], writes=[kk[2]])
            tt("dve", t3, PS[:, b0 + 3, :], g1, ALU.mult, reads=[("PS", b0 + 3), kk[1], RG_TT], writes=[kk[3]])
            tt("pool", MT[:, o, tsl(n)], t2, t3, ALU.add, reads=[kk[2], kk[3], RG_QR, RG_TT], writes=[("MT", o, n)])

        for pos in range(5, 8):
            g = WSEQ[pos]
            slot = pos % 2
            load_w(WSEQ[pos + 1], (pos + 1) % 2)
            for oo in range(2):
                for n in range(4):
                    inst = p3[0] % 2
                    p3[0] += 1
                    p3_unit(slot, g, oo, n, inst)

        def load_x(t):
            inst = t % 2
            s.op("sp", lambda e: e.dma_start(out=X4[inst], in_=d_x[t * 128:(t + 1) * 128, :]),
                 reads=[RG_TT], writes=[("X4", inst)], dma=True)

        load_x(0)
        load_x(1)

        def p4_front(t):
            inst = t % 2
            yb0 = 4 + (t % 2) * 2
            for h in range(2):
                for kc in range(8):
                    mm(PS[:, yb0 + h, :], MT[:, kc, tsl(t, 128)], WO[:, kc, tsl(h)], kc == 0, kc == 7,
                       reads=[("MT", kc, t // 4), ("WO",), RG_CW, RG_QR], writes=[("PS", yb0 + h)])
            r4, x4 = R4[t % 3], X4[inst]
            kr, kx = ("R4", t % 3), ("X4", inst)
            r4v = r4.rearrange("p (h f) -> p h f", h=2)
            x4v = x4.rearrange("p (h f) -> p h f", h=2)
            s.op("dve", lambda e: e.scalar_tensor_tensor(
                out=r4v, in0=x4v, scalar=DN_ALPHA, in1=PS[:, yb0:yb0 + 2, :], op0=ALU.mult, op1=ALU.add),
                reads=[kx, ("PS", yb0), ("PS", yb0 + 1), RG_TT, RG_QR], writes=[kr])
            if t + 2 < 16:
                load_x(t + 2)
            for h in range(2):
                s.op("dve", (lambda h: lambda e: e.bn_stats(out=STT[:, inst, 6 * h:6 * h + 6], in_=r4[:, tsl(h)]))(h),
                     reads=[kr], writes=[("STT", inst, h)])
            s.op("dve", lambda e: e.bn_aggr(out=MV[:, inst, :], in_=STT[:, inst, :]),
                 reads=[("STT", inst, 0), ("STT", inst, 1)], writes=[("MV", inst)])
            act(SM[:, inst, 0:1], MV[:, inst, 1:2], AF.Ln, reads=[("MV", inst)], writes=[("SM0", inst)],
                bias=LN_EPS, scale=1.0)
            act(SM[:, inst, 0:1], SM[:, inst, 0:1], AF.Exp, reads=[("SM0", inst)], writes=[("SM0", inst)], scale=-0.5)
            s.op("dve", lambda e: e.scalar_tensor_tensor(
                out=SM[:, inst, 1:2], in0=MV[:, inst, 0:1], scalar=-1.0, in1=SM[:, inst, 0:1],
                op0=ALU.mult, op1=ALU.mult),
                reads=[("MV", inst), ("SM0", inst)], writes=[("SM1", inst)])
            act(r4, r4, AF.Identity, reads=[kr, ("SM0", inst), ("SM1", inst)], writes=[kr],
                bias=SM[:, inst, 1:2], scale=SM[:, inst, 0:1])

        def p4_back(t):
            r4, o4 = R4[t % 3], O4[0]
            kr, ko = ("R4", t % 3), ("O4", 0)
            tt("pool", o4, r4, LNG[:], ALU.mult, reads=[kr, ("LNG",), RG_QR], writes=[ko])
            tt("dve", r4[:, 0:640], o4[:, 0:640], LNB[:, 0:640], ALU.add, reads=[ko, ("LNB",), RG_QR], writes=[kr])
            tt("pool", r4[:, 640:1024], o4[:, 640:1024], LNB[:, 640:1024], ALU.add, reads=[ko, ("LNB",), RG_QR],
               writes=[("R4b", t % 3)])
            s.op("sp", lambda e: e.dma_start(out=d_out[t * 128:(t + 1) * 128, :], in_=r4),
                 reads=[kr, ("R4b", t % 3), RG_QR], dma=True)

        def p4_step(t):
            p4_front(t)
            if t >= 1:
                p4_back(t - 1)

        g = WSEQ[8]
        slot = 8 % 2
        p3_unit(slot, g, 0, 0, 0)
        p3_unit(slot, g, 1, 0, 0)
        for n in range(1, 4):
            t0 = 4 * (n - 1)
            p3_unit(slot, g, 0, n, 0)
            p4_step(t0)
            p4_step(t0 + 1)
            p3_unit(slot, g, 1, n, 0)
            p4_step(t0 + 2)
            p4_step(t0 + 3)
        for t in range(12, 16):
            p4_step(t)
        p4_back(15)

        s.emit(es)
    return nc


def _pair_idx(base):
    idx = []
    for c in range(4):
        idx += list(range(base + 64 * c, base + 64 * c + 64))
        idx += list(range(base + 64 * (4 + c), base + 64 * (4 + c) + 64))
    return idx


def _win_cols():
    QA_, KA_, VA_, ZA_, QB_, KB_, VB_, ZB_, GA_, GB_ = 0, 512, 640, 768, 1280, 1792, 1920, 2048, 2560, 3584
    cols = []
    cols += _pair_idx(QB_)
    cols += list(range(KB_, KB_ + 128)) + list(range(KA_, KA_ + 128)) + list(range(VA_, VA_ + 128)) + list(
        range(VB_, VB_ + 128))
    cols += _pair_idx(QA_)
    cols += _pair_idx(ZA_)
    cols += _pair_idx(ZB_)
    for g in range(4):
        for base in (GA_, GB_):
            for oo in range(2):
                o = 2 * g + oo
                cols += list(range(base + o * 128, base + (o + 1) * 128))
    return np.array(cols, dtype=np.int64)


def _const_tables():
    p = np.arange(128)
    d = p % 64
    i = d % 32
    t = np.arange(SEQ, dtype=np.float64)
    freqs = 10000.0 ** (-np.arange(16, dtype=np.float64) / 16.0)
    f = np.where(i < 16, freqs[i % 16], freqs[(i - 16) % 16])
    pos = np.where((i < 16)[:, None], np.floor(t / 64.0)[None, :], np.mod(t, 64.0)[None, :])
    ang = pos * f[:, None]
    cos = np.cos(ang)
    sin = np.sin(ang) * np.where(d < 32, -1.0, 1.0)[:, None]
    sk = np.arange(128)[:, None]
    qq = np.arange(384)[None, :]
    dist = np.abs(qq - 128 - sk).astype(np.float64)
    bias = np.zeros((128, 8, 384), np.float64)
    for c in range(4):
        for h2 in range(2):
            h = c + 4 * h2
            slope = 2.0 ** (-8.0 * (h + 1) / 8.0)
            bias[:, 2 * c + h2, :] = np.where(dist <= 128, np.exp(-slope * dist), 0.0)
    ident = np.eye(128)
    bones = (p[:, None] // 64 == p[None, :] // 64).astype(np.float64) / 64.0
    sw = np.where(d < 32, p + 32, p - 32)
    perm = np.zeros((128, 128))
    perm[p, sw] = 1.0
    mats = np.concatenate([ident, bones, perm], axis=1)
    f32 = lambda a: np.ascontiguousarray(a, dtype=np.float32)
    return f32(cos), f32(sin), f32(bias.reshape(128, 8 * 384)), f32(mats)


_CACHE = {}


def kernel(x, w_in, b_gate, sink_a, qnorm_b, knorm_b, w_proj_a, w_proj_b, w_out, ln_g, ln_b):
    x = np.asarray(x, dtype=np.float32)
    w = np.asarray(w_in, dtype=np.float32)[0]
    win = np.ascontiguousarray(w[:, _win_cols()])
    rows = np.array(_pair_idx(0), dtype=np.int64)
    wpa = np.ascontiguousarray(np.asarray(w_proj_a, np.float32)[0][rows])
    wpb = np.ascontiguousarray(np.asarray(w_proj_b, np.float32)[0][rows])
    wo = np.ascontiguousarray(np.asarray(w_out, np.float32)[0])
    cvec = np.zeros((128, 32), np.float32)
    cvec[:, 0:16] = np.asarray(b_gate, np.float32)[0].reshape(16, 128).T
    sk = np.asarray(sink_a, np.float32)[0]
    for c in range(4):
        cvec[:64, 16 + c] = sk[c]
        cvec[64:, 16 + c] = sk[4 + c]
    cvec[:, 20] = np.tile(np.asarray(qnorm_b, np.float32)[0], 2)
    cvec[:, 21] = np.tile(np.asarray(knorm_b, np.float32)[0], 2)
    lng = np.ascontiguousarray(np.broadcast_to(np.asarray(ln_g, np.float32)[0][None, :], (128, DM)))
    lnb = np.ascontiguousarray(np.broadcast_to(np.asarray(ln_b, np.float32)[0][None, :], (128, DM)))
    cos, sin, bias, mats = _const_tables()
    if "nc" not in _CACHE:
        _CACHE["nc"] = build_program()
    nc = _CACHE["nc"]
    shared = {"win": win, "wpa": wpa, "wpb": wpb, "wo": wo, "cvec": cvec, "lng": lng, "lnb": lnb,
              "cos": cos, "sin": sin, "biasA": bias, "mats": mats}
    in_maps = []
    for b in range(8):
        m = dict(shared)
        m["xT"] = np.ascontiguousarray(x[b].T)
        m["x"] = np.ascontiguousarray(x[b])
        in_maps.append(m)
    res = run_bass_kernel_spmd(nc, in_maps, core_ids=list(range(8)))
    return np.stack([np.asarray(r["out"], dtype=np.float32) for r in res.results], axis=0)
```
